# Optimizing a Trainium2 kernel written in Bass

```python
import jax, jax.numpy as jnp
from jax import lax
import numpy as np

D_MODEL = 2048
BATCH = 2
SEQ = 16384
DEPTH = 2
DEC_BATCH = 8
DEC_SEQ = 4096
PAST_LEN = 128

GRID_W = 64

POOL_WIDTH = D_MODEL // 4
POOL_WINDOWS = (2, 4, 8, 16)
POOL_GROUPS = len(POOL_WINDOWS)
POOL_GC = POOL_WIDTH // POOL_GROUPS

HEAD_DIM = 128
N_Q_HEADS = (D_MODEL // 2) // HEAD_DIM
N_KV_HEADS = 2
GQA_GROUP = N_Q_HEADS // N_KV_HEADS
ATTN_WIDTH = N_Q_HEADS * HEAD_DIM
KV_WIDTH = N_KV_HEADS * HEAD_DIM
Q_BLOCK = 128
ROPE_THETA = 10000.0
ROPE_HALF = HEAD_DIM // 2
ROPE_FREQS_PER_AXIS = HEAD_DIM // 4

FOURIER_WIDTH = D_MODEL - POOL_WIDTH - ATTN_WIDTH
FOURIER_GROUPS = 4
FOURIER_GC = FOURIER_WIDTH // FOURIER_GROUPS

IN_WIDTH = POOL_WIDTH + ATTN_WIDTH + 2 * KV_WIDTH + FOURIER_WIDTH
IN_SPLITS = (POOL_WIDTH,
             POOL_WIDTH + ATTN_WIDTH,
             POOL_WIDTH + ATTN_WIDTH + KV_WIDTH,
             POOL_WIDTH + ATTN_WIDTH + 2 * KV_WIDTH)
MIX_WIDTH = POOL_WIDTH + ATTN_WIDTH + FOURIER_WIDTH

D_FF = 11 * D_MODEL // 4
CONV_WIDTH = 3

NORM_EPS = 1e-6

kernel_name = 'hybrid_pool_attn_fourier_encoder'


def rms_norm(x, g):
    xf = x.astype(jnp.float32)
    y = xf * lax.rsqrt(jnp.mean(xf * xf, axis=-1, keepdims=True) + NORM_EPS)
    return (y * g.astype(jnp.float32)).astype(x.dtype)


def axial_rope_tables(seq_len):
    rows = seq_len // GRID_W
    row_idx = jnp.repeat(jnp.arange(rows, dtype=jnp.float32), GRID_W)
    col_idx = jnp.tile(jnp.arange(GRID_W, dtype=jnp.float32), rows)
    inv_freq = 1.0 / (ROPE_THETA ** (jnp.arange(ROPE_FREQS_PER_AXIS, dtype=jnp.float32) / ROPE_FREQS_PER_AXIS))
    ang = jnp.concatenate([row_idx[:, None] * inv_freq[None, :],
                           col_idx[:, None] * inv_freq[None, :]], axis=-1)
    return jnp.cos(ang), jnp.sin(ang)


def apply_rope(x, cos, sin):
    extra = x.ndim - 3
    shape = (1, cos.shape[0]) + (1,) * extra + (cos.shape[1],)
    c = cos.reshape(shape)
    s = sin.reshape(shape)
    xf = x.astype(jnp.float32)
    x1, x2 = xf[..., :ROPE_HALF], xf[..., ROPE_HALF:]
    return jnp.concatenate([x1 * c - x2 * s, x2 * c + x1 * s], axis=-1).astype(x.dtype)


def multiscale_pool(u, pool_w, pool_scale):
    B, L, _ = u.shape
    uf = u.astype(jnp.float32)
    csum = jnp.concatenate([jnp.zeros((B, 1, POOL_WIDTH), jnp.float32),
                            jnp.cumsum(uf, axis=1)], axis=1)
    t = np.arange(L)
    outs = []
    for g, w in enumerate(POOL_WINDOWS):
        lo = np.clip(t - w // 2, 0, L - 1)
        hi = np.clip(t + w - 1 - w // 2, 0, L - 1)
        cnt = jnp.asarray((hi - lo + 1).astype(np.float32))[None, :, None]
        sl = slice(g * POOL_GC, (g + 1) * POOL_GC)
        cg = csum[..., sl]
        win_sum = jnp.take(cg, jnp.asarray(hi + 1), axis=1) - jnp.take(cg, jnp.asarray(lo), axis=1)
        outs.append(win_sum / cnt - uf[..., sl])
    d = jnp.stack(outs, axis=2).astype(u.dtype)
    y = jnp.einsum('blgc,gce->blge', d, pool_w).reshape(B, L, POOL_WIDTH)
    return y * pool_scale


def gqa_attention(q, k, v):
    B, L = q.shape[0], q.shape[1]
    n_blk = L // Q_BLOCK
    scale = HEAD_DIM ** -0.5
    qb = q.reshape(B, n_blk, Q_BLOCK, N_KV_HEADS, GQA_GROUP, HEAD_DIM).transpose(1, 0, 2, 3, 4, 5)

    def block(q_blk):
        s = jnp.einsum('bqkgd,bskd->bkgqs', q_blk, k,
                       preferred_element_type=jnp.float32) * scale
        p = jax.nn.softmax(s, axis=-1)
        return jnp.einsum('bkgqs,bskd->bqkgd', p.astype(v.dtype), v)

    o = lax.map(block, qb)
    return o.transpose(1, 0, 2, 3, 4, 5).reshape(B, L, ATTN_WIDTH)


def fourier_mix(u, fourier_w):
    B, L, _ = u.shape
    ug = u.astype(jnp.float32).reshape(B, L, FOURIER_GROUPS, FOURIER_GC)
    f = jnp.fft.fft2(ug, axes=(1, 3), norm='ortho').real.astype(u.dtype)
    return jnp.einsum('blgc,gce->blge', f, fourier_w).reshape(B, L, FOURIER_WIDTH)


def token_mixer(h, w_in, pool_w, pool_scale, q_norm, k_norm, fourier_w, w_out):
    B, L, _ = h.shape
    z = h @ w_in
    u_pool, q, k, v, u_four = jnp.split(z, IN_SPLITS, axis=-1)
    cos, sin = axial_rope_tables(L)
    q = apply_rope(rms_norm(q.reshape(B, L, N_KV_HEADS, GQA_GROUP, HEAD_DIM), q_norm), cos, sin)
    k = apply_rope(rms_norm(k.reshape(B, L, N_KV_HEADS, HEAD_DIM), k_norm), cos, sin)
    v = v.reshape(B, L, N_KV_HEADS, HEAD_DIM)
    heads = jnp.concatenate([multiscale_pool(u_pool, pool_w, pool_scale),
                             gqa_attention(q, k, v),
                             fourier_mix(u_four, fourier_w)], axis=-1)
    return heads @ w_out


def conv_gated_mlp(h, w_up, conv_w, conv_b, w_down):
    u = h @ w_up
    up = jnp.pad(u, ((0, 0), (1, 1), (0, 0)))
    c = up[:, :-2] * conv_w[0] + up[:, 1:-1] * conv_w[1] + up[:, 2:] * conv_w[2] + conv_b
    gate, val = jnp.split(c, 2, axis=-1)
    return (jax.nn.gelu(gate, approximate=True) * val) @ w_down


def run_trunk(x, g_pre_mix, g_post_mix, w_in, pool_w, pool_scale, q_norm, k_norm,
              fourier_w, w_out, g_pre_ffn, g_post_ffn, w_up, conv_w, conv_b, w_down):
    for l in range(DEPTH):
        m = token_mixer(rms_norm(x, g_pre_mix[l]), w_in[l], pool_w[l], pool_scale[l],
                        q_norm[l], k_norm[l], fourier_w[l], w_out[l])
        x = x + rms_norm(m, g_post_mix[l])
        f = conv_gated_mlp(rms_norm(x, g_pre_ffn[l]), w_up[l], conv_w[l], conv_b[l], w_down[l])
        x = x + rms_norm(f, g_post_ffn[l])
    return x


def setup_inputs(seed: int = 0) -> dict:
    key = jax.random.key(seed)
    ks = jax.random.split(key, 20)
    f32 = jnp.float32

    def nrm(k, shape, scale):
        return jax.random.normal(k, shape, f32) * scale

    def gain(k, shape):
        return 1.0 + 0.05 * jax.random.normal(k, shape, f32)

    return {
        'x_prompt': nrm(ks[0], (BATCH, SEQ, D_MODEL), 1.0),
        'x_sample': nrm(ks[1], (DEC_BATCH, DEC_SEQ, D_MODEL), 1.0),
        'g_pre_mix': gain(ks[2], (DEPTH, D_MODEL)),
        'g_post_mix': gain(ks[3], (DEPTH, D_MODEL)),
        'w_in': nrm(ks[4], (DEPTH, D_MODEL, IN_WIDTH), D_MODEL ** -0.5),
        'pool_w': nrm(ks[5], (DEPTH, POOL_GROUPS, POOL_GC, POOL_GC), POOL_GC ** -0.5),
        'pool_scale': gain(ks[6], (DEPTH, POOL_WIDTH)),
        'q_norm': gain(ks[7], (DEPTH, HEAD_DIM)),
        'k_norm': gain(ks[8], (DEPTH, HEAD_DIM)),
        'fourier_w': nrm(ks[9], (DEPTH, FOURIER_GROUPS, FOURIER_GC, FOURIER_GC), FOURIER_GC ** -0.5),
        'w_out': nrm(ks[10], (DEPTH, MIX_WIDTH, D_MODEL), MIX_WIDTH ** -0.5),
        'g_pre_ffn': gain(ks[11], (DEPTH, D_MODEL)),
        'g_post_ffn': gain(ks[12], (DEPTH, D_MODEL)),
        'w_up': nrm(ks[13], (DEPTH, D_MODEL, 2 * D_FF), D_MODEL ** -0.5),
        'conv_w': nrm(ks[14], (DEPTH, CONV_WIDTH, 2 * D_FF), CONV_WIDTH ** -0.5),
        'conv_b': nrm(ks[15], (DEPTH, 2 * D_FF), 0.02),
        'w_down': nrm(ks[16], (DEPTH, D_FF, D_MODEL), D_FF ** -0.5),
    }


def reference(x_prompt, x_sample, g_pre_mix, g_post_mix, w_in, pool_w, pool_scale, q_norm,
              k_norm, fourier_w, w_out, g_pre_ffn, g_post_ffn, w_up, conv_w, conv_b, w_down):
    y_prompt = run_trunk(x_prompt, g_pre_mix, g_post_mix, w_in, pool_w, pool_scale, q_norm,
                         k_norm, fourier_w, w_out, g_pre_ffn, g_post_ffn, w_up, conv_w,
                         conv_b, w_down)
    y_sample = run_trunk(x_sample, g_pre_mix, g_post_mix, w_in, pool_w, pool_scale, q_norm,
                         k_norm, fourier_w, w_out, g_pre_ffn, g_post_ffn, w_up, conv_w,
                         conv_b, w_down)
    return (y_prompt, y_sample)
```

```python
import numpy as np
import ml_dtypes
from contextlib import ExitStack
import concourse.bass as bass
import concourse.mybir as mybir
from concourse.bass_utils import run_bass_kernel_spmd

F32 = mybir.dt.float32
BF16 = mybir.dt.bfloat16
AF = mybir.ActivationFunctionType
ALU = mybir.AluOpType
NPBF = ml_dtypes.bfloat16

D = 2048
NT = 4096
DEPTH = 2
DFF = 5632
NJ = 44
EPS = 1e-6
GROWS = 8208
FFN_TILES = [456] * 8 + [448]


class Prog:
    NSLOT = {"sp": 20, "pool": 12}

    def __init__(self, nc, stack):
        self.nc = nc
        self.eng = {"pe": nc.tensor, "act": nc.scalar, "dve": nc.vector, "pool": nc.gpsimd, "sp": nc.sync}
        self.ops = []
        self.streams = {e: [] for e in self.eng}
        self.lw = {}
        self.rd = {}
        self.sem = {e: stack.enter_context(nc.semaphore("s_" + e)) for e in ("pe", "act", "dve", "pool")}
        self.dsem = {}
        for e, n in self.NSLOT.items():
            for i in range(n):
                self.dsem[(e, i)] = stack.enter_context(nc.semaphore(f"d_{e}{i}"))
        self.slot_rr = {e: 0 for e in self.NSLOT}
        self.slot_last = {}
        self.slot_cnt = {}
        self.stack = stack
        self.ncoll = 0

    def _add(self, eng, kind, fn, reads, writes, chan, extra_deps=()):
        oid = len(self.ops)
        deps = set(extra_deps)
        for k in reads:
            deps.update(self.lw.get(k, {}).values())
        for k in writes:
            deps.update(self.lw.get(k, {}).values())
            deps.update(self.rd.get(k, {}).values())
        for k in reads:
            self.rd.setdefault(k, {})[chan] = oid
        for k in writes:
            self.lw[k] = {chan: oid}
            self.rd[k] = {}
        self.ops.append(dict(id=oid, eng=eng, kind=kind, fn=fn, deps=deps, chan=chan, val=None, marked=False))
        self.streams[eng].append(oid)
        return oid

    def op(self, eng, fn, reads=(), writes=()):
        return self._add(eng, "c", fn, reads, writes, eng)

    def dma(self, eng, fn, reads=(), writes=(), n=None):
        nd = n if n is not None else getattr(fn, "n", 1)
        n = self.NSLOT[eng]
        s = self.slot_rr[eng]
        self.slot_rr[eng] = (s + 1) % n
        chan = ("d", eng, s)
        extra = []
        if chan in self.slot_last:
            extra.append(self.slot_last[chan])
        oid = self._add(eng, "d", fn, reads, writes, chan, extra)
        self.slot_cnt[chan] = self.slot_cnt.get(chan, 0) + nd
        self.ops[oid]["val"] = 16 * self.slot_cnt[chan]
        self.ops[oid]["nd"] = nd
        self.ops[oid]["semh"] = self.dsem[(eng, s)]
        self.slot_last[chan] = oid
        return oid

    def coll(self, fn, reads=(), writes=()):
        semh = self.stack.enter_context(self.nc.semaphore(f"cc{self.ncoll}"))
        chan = ("x", self.ncoll)
        self.ncoll += 1
        oid = self._add("pool", "x", fn, reads, writes, chan)
        self.ops[oid]["val"] = 1
        self.ops[oid]["semh"] = semh
        return oid

    def barrier(self):
        allk = list(set(self.lw.keys()) | set(self.rd.keys()))
        for e in ("pe", "act", "dve", "pool", "sp"):
            self._add(e, "n", None, allk, [], e)
        self.lw = {}
        self.rd = {}

    def finalize(self):
        ops = self.ops
        for o in ops:
            for d in o["deps"]:
                do = ops[d]
                if do["kind"] == "c":
                    if do["eng"] == "pe" and o["eng"] == "pe" and o["kind"] in ("c", "n"):
                        continue
                    do["marked"] = True
        cnt = {e: 0 for e in self.eng}
        for o in ops:
            if o["kind"] == "c" and o["marked"]:
                cnt[o["eng"]] += 1
                o["val"] = cnt[o["eng"]]

    def semof(self, o):
        if o["kind"] in ("c", "n"):
            return self.sem[o["eng"]]
        return o["semh"]

    def emit(self, ename, eng):
        ops = self.ops
        waited = {}
        for oid in self.streams[ename]:
            o = ops[oid]
            need = {}
            for d in o["deps"]:
                do = ops[d]
                if do["kind"] in ("c", "n"):
                    if do["eng"] == "pe" and ename == "pe" and o["kind"] in ("c", "n"):
                        continue
                semh = self.semof(do)
                key = id(semh)
                if need.get(key, (None, 0))[1] < do["val"]:
                    need[key] = (semh, do["val"])
            for key, (semh, v) in need.items():
                if waited.get(key, 0) < v:
                    eng.wait_ge(semh, v)
                    waited[key] = v
            if o["kind"] == "n":
                continue
            inst = o["fn"](eng)
            if o["kind"] == "c":
                if o["marked"]:
                    inst.then_inc(self.sem[ename], 1)
            elif o["kind"] == "d":
                insts = inst if isinstance(inst, list) else [inst]
                assert len(insts) == o["nd"], (len(insts), o["nd"])
                for ii in insts:
                    ii.then_inc(o["semh"], 16)
            else:
                inst.then_inc(o["semh"])


def _rope_tables(pos, seq_rows_w=64):
    pos = np.asarray(pos)
    row = (pos // 64).astype(np.float32)
    col = (pos % 64).astype(np.float32)
    inv = (1.0 / (np.float32(10000.0) ** (np.arange(32, dtype=np.float32) / np.float32(32)))).astype(np.float32)
    ang = np.concatenate([row[:, None] * inv[None, :], col[:, None] * inv[None, :]], axis=-1).astype(np.float32)
    c = np.cos(ang).astype(np.float32).T
    s = np.sin(ang).astype(np.float32).T
    C = np.concatenate([c, c], 0)
    S = np.concatenate([-s, s], 0)
    return np.ascontiguousarray(C), np.ascontiguousarray(S)


def _rcnt_table(pos, L):
    pos = np.asarray(pos)
    out = np.zeros((4, len(pos)), np.float32)
    for g, w in enumerate((2, 4, 8, 16)):
        lo = np.clip(pos - w // 2, 0, L - 1)
        hi = np.clip(pos + w - 1 - w // 2, 0, L - 1)
        out[g] = 1.0 / (hi - lo + 1).astype(np.float32)
    return out


def _g_table(L, k1_0):
    N1 = L // 128
    n1 = np.arange(N1, dtype=np.int64)[:, None, None]
    k2 = np.arange(128, dtype=np.int64)[None, :, None]
    j = np.arange(32, dtype=np.int64)[None, None, :]
    k = 128 * (k1_0 + j) + k2
    ph = ((n1 * k) % L).astype(np.float64) * (2.0 * np.pi / L)
    gc = np.cos(ph)
    gs = np.sin(ph)
    t = np.concatenate([gc, -gs, gs, gc], axis=-1)
    return t.astype(NPBF)


_CONST_CACHE = {}


def _consts():
    if _CONST_CACHE:
        return _CONST_CACHE
    c = _CONST_CACHE
    c["ident"] = np.eye(128, dtype=np.float32).astype(NPBF)
    c["ones_bf"] = np.ones((128, 128), np.float32).astype(NPBF)
    c["ones_f"] = np.ones((128, 128), np.float32)
    psw = np.zeros((128, 128), np.float32)
    for i in range(64):
        psw[i, i + 64] = 1.0
        psw[i + 64, i] = 1.0
    c["psw"] = psw
    n = np.arange(128, dtype=np.int64)
    ph = ((n[:, None] * n[None, :]) % 128).astype(np.float64) * (2 * np.pi / 128)
    c["dft128"] = np.concatenate([np.cos(ph), -np.sin(ph)], 1).astype(NPBF)
    c["cdft"] = np.concatenate([np.cos(ph), np.sin(ph)], 1).astype(NPBF)
    c["ropeC_s"], c["ropeS_s"] = _rope_tables(np.arange(4096))
    c["rcnt_s"] = _rcnt_table(np.arange(4096), 4096)
    c["g_s"] = _g_table(4096, 0)
    for r in range(4):
        pos = np.arange(4096) + 4096 * r
        c[f"ropeC_p{r}"], c[f"ropeS_p{r}"] = _rope_tables(pos)
        c[f"rcnt_p{r}"] = _rcnt_table(pos, 16384)
        c[f"g_p{r}"] = _g_table(16384, 32 * r)
    return c


def I_mm(out, lhsT, rhs, start=True, stop=True):
    return lambda e: e.matmul(out, lhsT=lhsT, rhs=rhs, start=start, stop=stop)


def I_mms(lst):
    def f(e):
        r = None
        for (out, lhsT, rhs, st, sp) in lst:
            r = e.matmul(out, lhsT=lhsT, rhs=rhs, start=st, stop=sp)
        return r
    return f


def I_trs(lst, ident):
    def f(e):
        r = None
        for (out, in_) in lst:
            r = e.transpose(out, in_, ident)
        return r
    return f


def I_act(out, in_, func, scale=None, bias=None, accum=None):
    kw = {}
    if scale is not None:
        kw["scale"] = scale
    if bias is not None:
        kw["bias"] = bias
    if accum is not None:
        kw["accum_out"] = accum
    return lambda e: e.activation(out=out, in_=in_, func=func, **kw)


def I_tt(out, in0, in1, op):
    return lambda e: e.tensor_tensor(out=out, in0=in0, in1=in1, op=op)


def I_ts(out, in0, s1, op0, s2=None, op1=None):
    if op1 is None:
        return lambda e: e.tensor_scalar(out=out, in0=in0, scalar1=s1, scalar2=None, op0=op0)
    return lambda e: e.tensor_scalar(out=out, in0=in0, scalar1=s1, scalar2=s2, op0=op0, op1=op1)


def I_stt(out, in0, scalar, in1, op0, op1):
    return lambda e: e.scalar_tensor_tensor(out=out, in0=in0, scalar=scalar, in1=in1, op0=op0, op1=op1)


def I_cp(out, in_):
    return lambda e: e.tensor_copy(out, in_)


def I_rcp(out, in_):
    return lambda e: e.reciprocal(out, in_)


def I_dma(out, in_):
    return lambda e: e.dma_start(out=out, in_=in_)


def I_dmas(lst):
    def f(e):
        return [e.dma_start(out=out, in_=in_) for (out, in_) in lst]
    f.n = len(lst)
    return f


def I_memset(out, v):
    return lambda e: e.memset(out, v)


class Arena:
    def __init__(self, t, n32):
        self.t = t
        self.n = n32
        self.off = 0
        self.marks = []

    def push(self):
        self.marks.append(self.off)

    def pop(self):
        self.off = self.marks.pop()

    def alloc(self, shape, dt):
        cols = int(np.prod(shape))
        n32 = (cols * (2 if dt == BF16 else 4) + 3) // 4
        n32 = (n32 + 7) // 8 * 8
        assert self.off + n32 <= self.n, f"arena overflow {self.off}+{n32}>{self.n}"
        v = self.t[:, self.off:self.off + n32]
        self.off += n32
        if dt != F32:
            v = v.bitcast(dt)
        v = v[:, 0:cols]
        if len(shape) == 2:
            v = v.rearrange("p (a b) -> p a b", b=shape[1])
        elif len(shape) == 3:
            v = v.rearrange("p (a b c) -> p a b c", b=shape[1], c=shape[2])
        return v


ARENA_F32 = 50176


class Builder:
    def __init__(self, debug=None):
        self.debug = debug or {}
        self.nc = nc = bass.Bass("TRN2", target_bir_lowering=False)
        self.stack = ExitStack()
        self.P = Prog(nc, self.stack)
        self.din = {}
        self.dout = {}
        self.scr = {}
        self.rgroups = [[0, 1, 2, 3], [4, 5, 6, 7]] if self.debug.get("ncores", 8) == 8 else [[0, 1, 2, 3]]

    def inp(self, name, shape, dt=F32):
        self.din[name] = self.nc.dram_tensor(name, list(shape), dt, kind="ExternalInput").ap()
        return self.din[name]

    def outp(self, name, shape, dt=F32):
        self.dout[name] = self.nc.dram_tensor(name, list(shape), dt, kind="ExternalOutput").ap()
        return self.dout[name]

    def scratch(self, name, shape, dt):
        if name in self.debug.get("dump", ()):
            t = self.nc.dram_tensor(name, list(shape), dt, kind="ExternalOutput")
        else:
            t = self.nc.dram_tensor(name, list(shape), dt)
        self.scr[name] = t
        return t.ap()

    def declare(self):
        i = self.inp
        i("xs", [NT, D]); i("xp", [NT, D])
        i("w_in", [DEPTH, D, 2560]); i("w_out", [DEPTH, D, D]); i("w_up", [DEPTH, D, 2 * DFF]); i("w_down", [DEPTH, DFF, D])
        i("pool_w", [DEPTH, 4, 128, 128]); i("fourier_w", [DEPTH, 4, 128, 128])
        i("gpreT", [DEPTH, 128, 16]); i("gffnT", [DEPTH, 128, 16]); i("gpost", [DEPTH, 1, D]); i("gpostfT", [DEPTH, 128, 16])
        i("qn", [DEPTH, 128, 4]); i("pscale", [DEPTH, 128, 4])
        i("convw", [DEPTH, 128, 3 * 88]); i("convb", [DEPTH, 128, 88])
        i("ident", [128, 128], BF16); i("ident_f", [128, 128]); i("ones_bf", [128, 128], BF16); i("ones_f", [128, 128]); i("psw", [128, 128])
        i("dft128", [128, 256], BF16); i("cdft", [128, 256], BF16)
        i("ropeC_s", [128, NT]); i("ropeS_s", [128, NT]); i("ropeC_p", [128, NT]); i("ropeS_p", [128, NT])
        i("rcnt_s", [4, NT]); i("rcnt_p", [4, NT])
        i("g_s", [32, 128 * 128], BF16); i("g_p", [128, 128 * 128], BF16)
        self.outp("ys", [NT, D]); self.outp("yp", [NT, D])
        s = self.scratch
        for l in range(DEPTH):
            s(f"wq_in{l}", [128, 16 * 2560], BF16); s(f"wq_out{l}", [128, 16 * D], BF16)
            s(f"wq_up{l}", [NJ, 128, 16 * 256], BF16); s(f"wq_down{l}", [16, 128, NJ * 128], BF16)
        for p in "sp":
            s(f"xcur_{p}", [NT, D], F32); s(f"xme_{p}", [NT + 2, D], F32)
            s(f"upe_{p}", [512, NT + 16], BF16); s(f"qT_{p}", [1024, NT], BF16)
            s(f"attnT_{p}", [1024, NT], BF16); s(f"fourT_{p}", [512, NT], BF16)
        s("kT_s", [256, NT], BF16); s("v_s", [NT, 256], BF16); s("f_s", [NT, 512], BF16)
        for k in range(8):
            s(f"gin{k}", [1024, 512], BF16); s(f"gout{k}", [4096, 512], BF16)
        s("gin8", [16, 512], BF16); s("gout8", [64, 512], BF16)
        s("fin", [16, 512], BF16); s("fout", [64, 512], BF16)
        s("ph", [6, 8192], BF16)
        s("xe_in", [2, D], F32); s("xe_out", [8, D], F32); s("xpd", [12, D], F32)
        s("A_s", [2, 128, 32 * 512], BF16); s("A_p", [2, 128, 128 * 512], BF16)

    def build(self):
        nc, st, P = self.nc, self.stack, self.P
        self.declare()
        arena_t = st.enter_context(nc.sbuf_tensor("arena", [128, ARENA_F32], F32))
        self.ar = ar = Arena(arena_t, ARENA_F32)
        self.ps = [st.enter_context(nc.psum_tensor(f"ps{i}", [128, 512], F32)) for i in range(8)]
        self.ident = ar.alloc([128], BF16); self.ones_bf = ar.alloc([128], BF16)
        self.ones_f = ar.alloc([128], F32); self.psw = ar.alloc([128], F32); self.ident_f = ar.alloc([128], F32)
        self.dft128 = ar.alloc([256], BF16); self.cdft = ar.alloc([256], BF16)
        self.epsb = ar.alloc([1], F32)
        self.small = ar.alloc([64], F32)
        d = self.din
        P.dma("sp", I_dmas([(self.ident, d["ident"]), (self.ones_bf, d["ones_bf"]), (self.ones_f, d["ones_f"]),
                            (self.psw, d["psw"]), (self.ident_f, d["ident_f"]), (self.dft128, d["dft128"]), (self.cdft, d["cdft"])]), [], ["consts"])
        P.op("dve", I_memset(self.epsb, EPS), [], ["epsb"])
        ar.push()
        self.zero = ar.alloc([2048], F32)
        P.op("dve", I_memset(self.zero, 0.0), [], ["zero"])
        self.prologue()
        P.barrier()
        ar.pop()
        for l in range(self.debug.get("layers", DEPTH)):
            self.layer(l)
        P.barrier()
        P.finalize()
        with nc.Block() as block:
            @block.tensor
            def _(e):
                P.emit("pe", e)

            @block.scalar
            def _(e):
                P.emit("act", e)

            @block.vector
            def _(e):
                P.emit("dve", e)

            @block.gpsimd
            def _(e):
                self.rank = e.partition_id() % 4
                P.emit("pool", e)

            @block.sync
            def _(e):
                P.emit("sp", e)
        return nc

    def prologue(self):
        P, d, s = self.P, self.din, self.scr
        z = self.zero
        zb = z.bitcast(BF16)
        for p in "sp":
            upe = s[f"upe_{p}"].ap()
            P.dma("pool", I_dmas([(upe[128 * k:128 * k + 128, 0:8], zb[:, 0:8]) for k in range(4)] +
                                 [(upe[128 * k:128 * k + 128, NT + 8:NT + 16], zb[:, 0:8]) for k in range(4)]),
                  ["zero"], [("upe_hl", p), ("upe_hr", p)])
            xme = s[f"xme_{p}"].ap()
            P.dma("pool", I_dmas([(xme[0:1, :], z[0:1, :]), (xme[NT + 1:NT + 2, :], z[0:1, :])]), ["zero"],
                  [("xme_hl", p), ("xme_hr", p)])
        ph = s["ph"].ap()
        P.dma("pool", I_dmas([(ph[0:1, 0:4096], zb[0:1, 0:4096]), (ph[0:1, 4096:8192], zb[0:1, 0:4096]),
                              (ph[5:6, 0:4096], zb[0:1, 0:4096]), (ph[5:6, 4096:8192], zb[0:1, 0:4096])]),
              ["zero"], ["ph_pad"])
        xpd = s["xpd"].ap()
        P.dma("pool", I_dmas([(xpd[0:2, :], z[0:2, :]), (xpd[10:12, :], z[0:2, :])]), ["zero"], ["xpd_pad"])
        self.bgq = []
        self.bgdone = set()
        q = self.bgq
        for l in range(DEPTH):
            wq = s[f"wq_in{l}"].ap().rearrange("p (k c) -> p k c", c=2560)
            for h in range(4):
                q.append((("wq_in", l, h), I_dma(wq[:, 4 * h:4 * h + 4, :],
                          d["w_in"][l, 512 * h:512 * h + 512, :].rearrange("(k p) c -> p k c", p=128))))
            wq = s[f"wq_out{l}"].ap().rearrange("p (k c) -> p k c", c=D)
            for h in range(4):
                q.append((("wq_out", l, h), I_dma(wq[:, 4 * h:4 * h + 4, :],
                          d["w_out"][l, 512 * h:512 * h + 512, :].rearrange("(k p) c -> p k c", p=128))))
            wq = s[f"wq_up{l}"].ap().rearrange("j p (k c) -> j p k c", c=256)
            for j in range(NJ):
                q.append((("wq_up", l, j, 0), I_dma(wq[j, :, :, 0:128],
                          d["w_up"][l, :, 128 * j:128 * j + 128].rearrange("(k p) c -> p k c", p=128))))
                q.append((("wq_up", l, j, 1), I_dma(wq[j, :, :, 128:256],
                          d["w_up"][l, :, DFF + 128 * j:DFF + 128 * j + 128].rearrange("(k p) c -> p k c", p=128))))
            wq = s[f"wq_down{l}"].ap().rearrange("a p (j c) -> a p j c", c=128)
            for dc in range(16):
                q.append((("wq_down", l, dc), I_dma(wq[dc], d["w_down"][l, :, 128 * dc:128 * dc + 128].rearrange("(j p) c -> p j c", p=128))))
        self.bgpos = 0

    def bg(self, n=1):
        while n > 0 and self.bgpos < len(self.bgq):
            key, fn = self.bgq[self.bgpos]
            self.P.dma("pool", fn, [], [key])
            self.bgdone.add(key)
            self.bgpos += 1
            n -= 1

    def need(self, key):
        while key not in self.bgdone:
            assert self.bgpos < len(self.bgq), key
            self.bg(1)

    def part(self, p, l):
        d, s = self.din, self.scr
        o = dict(name=p)
        if l == 0:
            o["xsrc"] = d["xs"] if p == "s" else d["xp"]
        else:
            o["xsrc"] = s[f"xcur_{p}"].ap()
        o["xdst"] = (s[f"xcur_{p}"].ap() if l == 0 else self.dout["ys" if p == "s" else "yp"])
        o["ropeC"] = d[f"ropeC_{p}"]; o["ropeS"] = d[f"ropeS_{p}"]; o["rcnt"] = d[f"rcnt_{p}"]
        o["upe"] = s[f"upe_{p}"].ap(); o["qT"] = s[f"qT_{p}"].ap(); o["xme"] = s[f"xme_{p}"].ap()
        o["attnT"] = s[f"attnT_{p}"].ap(); o["fourT"] = s[f"fourT_{p}"].ap()
        if p == "s":
            kT, v, f = s["kT_s"].ap(), s["v_s"].ap(), s["f_s"].ap()
            o["kT_h"] = [kT[0:128, :], kT[128:256, :]]
            o["v_rows"] = lambda r0: v[r0:r0 + 128, :]
            o["f_rows"] = lambda r0: f[r0:r0 + 128, :]
            o["L"] = NT; o["A"] = s["A_s"].ap(); o["G"] = d["g_s"]
        else:
            gin = [s[f"gin{k}"].ap() for k in range(9)]
            o["kT_h"] = [gin[g].rearrange("(a b) c -> a (b c)", b=8) for g in range(2)]
            o["v_rows"] = lambda r0: gin[2 + r0 // 2048].rearrange("r (two c) -> (r two) c", two=2)[r0 % 2048:r0 % 2048 + 128, :]
            o["f_rows"] = lambda r0: gin[4 + r0 // 1024][r0 % 1024:r0 % 1024 + 128, :]
            o["L"] = 4 * NT; o["A"] = s["A_p"].ap(); o["G"] = d["g_p"]
        return o

    def layer(self, l):
        P = self.P
        ph = self.debug.get("phases", "AXBFCYD")
        parts = self.debug.get("parts", "sp")
        if "A" in ph:
            for p in "ps":
                if p in parts:
                    self.phase_A(l, self.part(p, l))
                    P.barrier()
        if "X" in ph:
            self.exchange1(l)
            P.barrier()
        for p in "sp":
            if p not in parts:
                continue
            pt = self.part(p, l)
            if "B" in ph:
                self.phase_attn(l, pt)
                P.barrier()
            if "F" in ph:
                self.phase_four(l, pt)
                P.barrier()
            if "C" in ph:
                self.phase_C(l, pt)
                P.barrier()
        if "Y" in ph:
            self.exchange2(l)
            P.barrier()
        if "D" in ph:
            for p in "sp":
                if p in parts:
                    self.phase_D(l, self.part(p, l))
                    P.barrier()

    def phase_A(self, l, pt):
        P, ar, d, ps = self.P, self.ar, self.din, self.ps
        p = pt["name"]
        ar.push()
        win = ar.alloc([16, 2560], BF16)
        gcol = ar.alloc([16], F32)
        qn = ar.alloc([4], F32)
        xt = [ar.alloc([D], F32) for _ in range(2)]
        junk = ar.alloc([D], BF16)
        xn = [ar.alloc([D], BF16) for _ in range(2)]
        hT = [ar.alloc([16, 512], BF16) for _ in range(2)]
        st4 = ar.alloc([16], F32)
        rC = [ar.alloc([512], F32) for _ in range(2)]
        rS = [ar.alloc([512], F32) for _ in range(2)]
        tmp = [[ar.alloc([512], F32) for _ in range(5)] for _ in range(2)]
        stg = [ar.alloc([512], BF16) for _ in range(4)]
        vf = [ar.alloc([768], BF16) for _ in range(2)]
        psT = [ps[0][:].bitcast(BF16), ps[1][:].bitcast(BF16)]
        for h in range(4):
            self.need(("wq_in", l, h))
        wq = self.scr[f"wq_in{l}"].ap().rearrange("p (k c) -> p k c", c=2560)
        for h in range(4):
            P.dma("sp", I_dma(win[:, 4 * h:4 * h + 4, :], wq[:, 4 * h:4 * h + 4, :]), [("wq_in", l, h)], [("win", h)])
        P.dma("sp", I_dmas([(gcol, d["gpreT"][l]), (qn, d["qn"][l])]), [], ["gcol"])
        wink = [("win", h) for h in range(4)]
        NTL = NT // 512
        sidx = [0]

        def prep(i, s):
            k = sidx[0]; sidx[0] += 1
            xs, xb, hh = xt[k % 2], xn[k % 2], hT[i % 2]
            r0 = 512 * i + 128 * s
            c0 = 4 * (k % 4)
            P.dma("sp", I_dma(xs, pt["xsrc"][r0:r0 + 128, :]), [], [("xt", k % 2)])
            P.op("act", I_act(junk, xs, AF.Square, accum=st4[:, c0:c0 + 1]), [("xt", k % 2)], ["junk", ("st", k % 4)])
            P.op("act", I_act(st4[:, c0 + 1:c0 + 2], st4[:, c0:c0 + 1], AF.Sqrt, scale=1.0 / D, bias=self.epsb),
                 [("st", k % 4), "epsb"], [("st", k % 4)])
            P.op("dve", I_rcp(st4[:, c0 + 2:c0 + 3], st4[:, c0 + 1:c0 + 2]), [("st", k % 4)], [("st", k % 4)])
            P.op("act", I_act(xb, xs, AF.Copy, scale=st4[:, c0 + 2:c0 + 3]), [("xt", k % 2), ("st", k % 4)], [("xn", k % 2)])
            for b in range(2):
                P.op("pe", I_trs([(psT[b][:, 128 * j:128 * j + 128], xb[:, 128 * (8 * b + j):128 * (8 * b + j) + 128])
                                  for j in range(8)], self.ident), [("xn", k % 2), "consts"], [("ps", b)])
                for j in range(8):
                    kc = 8 * b + j
                    P.op("dve", I_ts(hh[:, kc, 128 * s:128 * s + 128], psT[b][:, 128 * j:128 * j + 128], gcol[:, kc:kc + 1], ALU.mult),
                         [("ps", b), "gcol"], [("hT", i % 2, s, kc)])

        def tables(i):
            P.dma("sp", I_dmas([(rC[i % 2], pt["ropeC"][:, 512 * i:512 * i + 512]),
                                (rS[i % 2], pt["ropeS"][:, 512 * i:512 * i + 512])]), [], [("rope", i % 2)])

        for s_ in range(4):
            prep(0, s_)
        tables(0)
        cidx = [0]
        sgi = [0]
        for i in range(NTL):
            hh = hT[i % 2]
            hk = [("hT", i % 2, s_, kc) for s_ in range(4) for kc in range(16)]
            T0 = 512 * i
            if i + 1 < NTL:
                tables(i + 1)
            pend = None
            order = list(range(4, 14)) + list(range(0, 4))
            for n_, oc in enumerate(order):
                c = cidx[0]; cidx[0] += 1
                zb = 2 + (c % 2)
                P.op("pe", I_mms([(ps[zb][:], win[:, kc, 128 * oc:128 * oc + 128], hh[:, kc, :], kc == 0, kc == 15)
                                  for kc in range(16)]), hk + wink, [("ps", zb)])
                if pend is not None:
                    pend()
                    pend = None
                if oc < 4:
                    sg = sgi[0] % 4; sgi[0] += 1
                    P.op("act", I_act(stg[sg], ps[zb][:], AF.Copy), [("ps", zb)], [("stg", sg)])
                    P.dma("pool", I_dma(pt["upe"][128 * oc:128 * oc + 128, 8 + T0:8 + T0 + 512], stg[sg]), [("stg", sg)],
                          [("upe", p, oc, i)])
                else:
                    zs, sq, sd, t1, t2 = tmp[c % 2]
                    tk = ("tmp", c % 2)
                    isq = oc < 12
                    gsc = qn[:, 0:1] if isq else qn[:, 2:3]
                    gscP = qn[:, 1:2] if isq else qn[:, 3:4]
                    P.op("act", I_act(zs, ps[zb][:], AF.Copy), [("ps", zb)], [tk])
                    P.op("dve", I_tt(sq, zs, zs, ALU.mult), [tk], [(tk, "sq")])

                    def fin(zs=zs, sq=sq, sd=sd, t1=t1, t2=t2, tk=tk, gsc=gsc, gscP=gscP, oc=oc, isq=isq, i=i, T0=T0):
                        P.op("pe", I_mm(ps[4][:], self.ones_f, sq), [(tk, "sq"), "consts"], [("ps", 4)])
                        P.op("pe", I_mm(ps[5][:], self.psw, zs), [tk, "consts"], [("ps", 5)])
                        P.op("act", I_act(sd, ps[4][:], AF.Sqrt, scale=1.0 / 128, bias=self.epsb), [("ps", 4), "epsb"], [(tk, "sd")])
                        P.op("dve", I_rcp(sd, sd), [(tk, "sd")], [(tk, "sd")])
                        P.op("dve", I_stt(t1, zs, gsc, rC[i % 2], ALU.mult, ALU.mult), [tk, "gcol", ("rope", i % 2)], [(tk, "t1")])
                        P.op("dve", I_stt(t2, ps[5][:], gscP, rS[i % 2], ALU.mult, ALU.mult), [("ps", 5), "gcol", ("rope", i % 2)],
                             [(tk, "t2")])
                        P.op("dve", I_tt(t1, t1, t2, ALU.add), [(tk, "t1"), (tk, "t2")], [(tk, "t1")])
                        sg = sgi[0] % 4; sgi[0] += 1
                        P.op("dve", I_tt(stg[sg], t1, sd, ALU.mult), [(tk, "t1"), (tk, "sd")], [("stg", sg)])
                        if isq:
                            h = oc - 4
                            P.dma("pool", I_dma(pt["qT"][128 * h:128 * h + 128, T0:T0 + 512], stg[sg]), [("stg", sg)],
                                  [("qT", p, h, i)])
                        else:
                            h = oc - 12
                            P.dma("pool", I_dma(pt["kT_h"][h][:, T0:T0 + 512], stg[sg]), [("stg", sg)],
                                  [("kT", p, h, i)])
                    pend = fin
                if i + 1 < NTL and n_ in (2, 5, 8, 11):
                    prep(i + 1, (n_ - 2) // 3)
            if pend is not None:
                pend()
            for s_ in range(4):
                P.op("pe", I_mms([(ps[6][:], hh[:, kc, 128 * s_:128 * s_ + 128], win[:, kc, 1792:2304], kc == 0, kc == 15)
                                  for kc in range(16)] +
                                 [(ps[7][:, 0:256], hh[:, kc, 128 * s_:128 * s_ + 128], win[:, kc, 2304:2560], kc == 0, kc == 15)
                                  for kc in range(16)]), [("hT", i % 2, s_, kc) for kc in range(16)] + wink, [("ps", 6), ("ps", 7)])
                vv = vf[s_ % 2]
                P.op("act", I_act(vv[:, 0:512], ps[6][:], AF.Copy), [("ps", 6)], [("vf", s_ % 2, 0)])
                P.op("dve", I_cp(vv[:, 512:768], ps[7][:, 0:256]), [("ps", 7)], [("vf", s_ % 2, 1)])
                r0 = T0 + 128 * s_
                P.dma("pool", I_dmas([(pt["v_rows"](r0), vv[:, 0:256]), (pt["f_rows"](r0), vv[:, 256:768])]),
                      [("vf", s_ % 2, 0), ("vf", s_ % 2, 1)], [("vfd", p, i, s_)])
            self.bg(2)
        ar.pop()

    def exchange1(self, l):
        P, s = self.P, self.scr
        upe, ph = s["upe_p"].ap(), s["ph"].ap()
        hv = s["gin8"].ap().rearrange("(s a) (b e) -> s (a b) e", s=2, b=64, e=8)
        P.dma("pool", I_dmas([(hv[0], upe[:, 8:16]), (hv[1], upe[:, NT:NT + 8])]), ["all"], [("gin", 8)])
        for k in range(9):
            gi, go = s[f"gin{k}"], s[f"gout{k}"]
            P.coll((lambda gi, go: (lambda e: e.collective_compute(
                "AllGather", ALU.bypass, replica_groups=self.rgroups,
                ins=[gi.ap().opt()], outs=[go.ap().opt()])))(gi, go), [("gin", k)], [("gout", k)])
        self.fence([("gout", k) for k in range(9)])
        go8 = s["gout8"].ap()
        P.dma("pool", I_dmas([(ph[r + 1:r + 2, :], go8[16 * r:16 * r + 16, :].rearrange("(o a) b -> o (a b)", o=1))
                              for r in range(4)]), [("gout", 8), "ph_pad", "fence"], ["ph"])

        def halos(e):
            rk = self.rank
            return [e.dma_start(out=upe[:, 0:8], in_=ph[bass.ds(rk, 1), 4096:8192].rearrange("o (c e) -> (o c) e", e=8)),
                    e.dma_start(out=upe[:, NT + 8:NT + 16], in_=ph[bass.ds(rk + 2, 1), 0:4096].rearrange("o (c e) -> (o c) e", e=8))]
        P.dma("pool", halos, ["ph"], [("upe_hl", "p"), ("upe_hr", "p")], n=2)

    def fence(self, keys):
        P, s = self.P, self.scr
        fi, fo = s["fin"], s["fout"]
        P.coll(lambda e: e.collective_compute("AllGather", ALU.bypass, replica_groups=self.rgroups,
                                              ins=[fi.ap().opt()], outs=[fo.ap().opt()]), list(keys), ["fence"])

    def exchange2(self, l):
        P, s = self.P, self.scr
        xme = s["xme_p"].ap()
        xin_t, xout_t = s["xe_in"], s["xe_out"]
        xpd = s["xpd"].ap()
        P.dma("pool", I_dmas([(xin_t.ap()[0:1, :], xme[1:2, :]), (xin_t.ap()[1:2, :], xme[NT:NT + 1, :])]), ["all"], ["xe_in"])
        P.coll(lambda e: e.collective_compute("AllGather", ALU.bypass, replica_groups=self.rgroups,
                                              ins=[xin_t.ap().opt()], outs=[xout_t.ap().opt()]), ["xe_in"], ["xe_out"])
        self.fence(["xe_out"])
        P.dma("pool", I_dma(xpd[2:10, :], xout_t.ap()), ["xe_out", "xpd_pad", "fence"], ["xpd"])

        def halos(e):
            rk = self.rank
            return [e.dma_start(out=xme[0:1, :], in_=xpd[bass.ds(2 * rk + 1, 1), :]),
                    e.dma_start(out=xme[NT + 1:NT + 2, :], in_=xpd[bass.ds(2 * rk + 4, 1), :])]
        P.dma("pool", halos, ["xpd"], [("xme_hl", "p"), ("xme_hr", "p")], n=2)

    def phase_attn(self, l, pt):
        P, ar, ps, s = self.P, self.ar, self.ps, self.scr
        p = pt["name"]
        L = pt["L"]
        nkb = L // 128
        nr = L // NT
        ar.push()
        KT = ar.alloc([2, L], BF16)
        V = ar.alloc([2, nkb, 128], BF16)
        qt = [ar.alloc([512], BF16) for _ in range(3)]
        pT = [ar.alloc([512], BF16) for _ in range(3)]
        rden = [ar.alloc([512], F32) for _ in range(2)]
        ostg = [ar.alloc([512], BF16) for _ in range(2)]
        for g in range(2):
            if p == "s":
                P.dma("sp", I_dmas([(KT[:, g, :], pt["kT_h"][g]),
                                    (V[:, g, :, :], s["v_s"].ap()[:, 128 * g:128 * g + 128].rearrange("(kb p) c -> p kb c", p=128))]),
                      ["all"], [("KV", g)])
            else:
                lst = []
                for r in range(4):
                    lst.append((KT[:, g, NT * r:NT * r + NT],
                                s[f"gout{g}"].ap()[1024 * r:1024 * r + 1024, :].rearrange("(a b) c -> a (b c)", b=8)))
                    for hf in range(2):
                        vsrc = s[f"gout{2 + hf}"].ap()[1024 * r:1024 * r + 1024, :].rearrange("r (two c) -> (r two) c", two=2)
                        lst.append((V[:, g, 32 * r + 16 * hf:32 * r + 16 * hf + 16, :],
                                    vsrc[:, 128 * g:128 * g + 128].rearrange("(kb p) c -> p kb c", p=128)))
                P.dma("sp", I_dmas(lst), ["all"], [("KV", g)])
        scale = float(128 ** -0.5)
        u = 0
        for g in range(2):
            for h in range(4):
                hd = 4 * g + h
                for qg in range(NT // 512):
                    q = qt[u % 3]
                    P.dma("sp", I_dma(q, pt["qT"][128 * hd:128 * hd + 128, 512 * qg:512 * qg + 512]), ["all"], [("qt", u % 3)])
                    bo, bd = 3 + (u % 2), 5 + (u % 2)

                    def S(kb, q=q, g=g, u=u):
                        P.op("pe", I_mm(ps[kb % 3][:], KT[:, g, 128 * kb:128 * kb + 128], q), [("KV", g), ("qt", u % 3)], [("ps", kb % 3)])

                    def E(kb):
                        P.op("act", I_act(pT[kb % 3], ps[kb % 3][:], AF.Exp, scale=scale), [("ps", kb % 3)], [("pT", kb % 3)])

                    def PV(kb, g=g, bo=bo, bd=bd):
                        P.op("pe", I_mms([(ps[bo][:], V[:, g, kb, :], pT[kb % 3], kb == 0, kb == nkb - 1),
                                          (ps[bd][:], self.ones_bf, pT[kb % 3], kb == 0, kb == nkb - 1)]),
                             [("KV", g), ("pT", kb % 3), "consts"], [("ps", bo), ("ps", bd)])
                    S(0); S(1)
                    for kb in range(nkb):
                        E(kb)
                        if kb + 2 < nkb:
                            S(kb + 2)
                        PV(kb)
                    rd, og = rden[u % 2], ostg[u % 2]
                    P.op("dve", I_rcp(rd, ps[bd][:]), [("ps", bd)], [("rden", u % 2)])
                    P.op("dve", I_tt(og, ps[bo][:], rd, ALU.mult), [("ps", bo), ("rden", u % 2)], [("ostg", u % 2)])
                    P.dma("pool", I_dma(pt["attnT"][128 * hd:128 * hd + 128, 512 * qg:512 * qg + 512], og), [("ostg", u % 2)],
                          [("attnT", p, hd, qg)])
                    u += 1
            self.bg(8)
        ar.pop()

    def phase_four(self, l, pt):
        P, ar, ps, s, d = self.P, self.ar, self.ps, self.scr, self.din
        p = pt["name"]
        L = pt["L"]
        N1 = L // 128
        ar.push()
        Gt = ar.alloc([128, 128], BF16)
        ZT = ar.alloc([4, 2, NT], BF16)
        X = [ar.alloc([4, 512], BF16) for _ in range(3)]
        As = [ar.alloc([2, 512], BF16) for _ in range(3)]
        Ak = [ar.alloc([2, 512], BF16) for _ in range(3)]
        fw = ar.alloc([4, 128], BF16)
        M = ar.alloc([4, 2, 128], BF16)
        ostg = [ar.alloc([512], BF16) for _ in range(2)]
        if N1 < 128:
            P.op("dve", I_memset(Gt.rearrange("p a b -> p (a b)"), 0.0), [], ["Gt"])
            for i_ in range(3):
                P.op("dve", I_memset(Ak[i_].rearrange("p a b -> p (a b)"), 0.0), [], [("Ak", i_)])
        P.dma("sp", I_dma(Gt[0:N1], pt["G"].rearrange("n (k c) -> n k c", c=128)), [], ["Gt"])
        P.dma("pool", I_dma(fw, d["fourier_w"][l].rearrange("g c e -> c g e")), [], ["fw"])
        for g in range(4):
            for t in range(2):
                P.op("pe", I_mm(ps[6 + t][:, 0:128], self.cdft[:, 128 * t:128 * t + 128], fw[:, g, :]), ["fw", "consts"], [("ps", 6 + t)])
                P.op("dve", I_cp(M[:, g, t, :], ps[6 + t][:, 0:128]), [("ps", 6 + t)], ["M"])
        Av = pt["A"].rearrange("t k (n c) -> t k n c", c=512)
        fstop = self.debug.get("fstop", 9)
        if fstop <= 1:
            ar.pop(); return
        k = 0
        for q in range(N1 // 4):
            x = X[q % 3]
            if p == "s":
                lst = [(x, s["f_s"].ap().rearrange("(n2 n1) c -> n2 n1 c", n1=32)[:, 4 * q:4 * q + 4, :])]
            else:
                lst = [(x[32 * r + 8 * c_:32 * r + 8 * c_ + 8],
                        s[f"gout{4 + c_}"].ap()[1024 * r:1024 * r + 1024, :].rearrange("(j n1) c -> j n1 c", n1=128)[:, 4 * q:4 * q + 4, :])
                       for r in range(4) for c_ in range(4)]
            P.dma("sp", I_dmas(lst), ["all"], [("X", q % 3)])
            for j in range(4):
                n1 = 4 * q + j
                a = As[k % 3]
                P.op("pe", I_mm(ps[0 + 2 * (k % 2)][:], self.dft128[:, 0:128], x[:, j, :]), [("X", q % 3), "consts"], [("ps", 2 * (k % 2))])
                P.op("pe", I_mm(ps[1 + 2 * (k % 2)][:], self.dft128[:, 128:256], x[:, j, :]), [("X", q % 3), "consts"], [("ps", 1 + 2 * (k % 2))])
                P.op("act", I_act(a[:, 0, :], ps[0 + 2 * (k % 2)][:], AF.Copy), [("ps", 2 * (k % 2))], [("As", k % 3, 0)])
                P.op("dve", I_cp(a[:, 1, :], ps[1 + 2 * (k % 2)][:]), [("ps", 1 + 2 * (k % 2))], [("As", k % 3, 1)])
                P.dma("pool", I_dmas([(Av[0, :, n1, :], a[:, 0, :]), (Av[1, :, n1, :], a[:, 1, :])]),
                      [("As", k % 3, 0), ("As", k % 3, 1)], [("A", n1)])
                k += 1
        allA = [("A", n1) for n1 in range(N1)]
        if fstop <= 2:
            ar.pop(); return
        ev = 0
        for k2 in range(128):
            a = Ak[k2 % 3]
            P.dma("sp", I_dmas([(a[0:N1, 0, :], Av[0, k2, :, :]), (a[0:N1, 1, :], Av[1, k2, :, :])]), allA, [("Ak", k2 % 3)])
            for g in range(4):
                b = 4 * (k2 % 2) + g
                P.op("pe", I_mms([(ps[b][:, 0:64], a[:, 0, 128 * g:128 * g + 128], Gt[:, k2, 0:64], True, False),
                                  (ps[b][:, 0:64], a[:, 1, 128 * g:128 * g + 128], Gt[:, k2, 64:128], False, True)]),
                     [("Ak", k2 % 3), "Gt"], [("ps", b)])
            for g in range(4):
                if self.debug.get("f2") == "mm":
                    break
                b = 4 * (k2 % 2) + g
                src = ps[b][:, 0:64].rearrange("p (t k) -> p t k", k=32)
                dst = ZT[:, g, :, :].rearrange("p t (k o) -> p t k o", o=128)[:, :, :, k2]
                if ev % 2 == 0:
                    P.op("act", I_act(dst, src, AF.Copy), [("ps", b)], [("ZT", g, k2)])
                else:
                    P.op("dve", I_cp(dst, src), [("ps", b)], [("ZT", g, k2)])
                ev += 1
        if fstop <= 3:
            ar.pop(); return
        sc = float(1.0 / np.sqrt(128.0 * L))
        k = 0
        for g in range(4):
            zk = [("ZT", g, k2) for k2 in range(128)]
            for tb in range(NT // 512):
                b = 6 + k % 2
                P.op("pe", I_mms([(ps[b][:], M[:, g, 0, :], ZT[:, g, 0, 512 * tb:512 * tb + 512], True, False),
                                  (ps[b][:], M[:, g, 1, :], ZT[:, g, 1, 512 * tb:512 * tb + 512], False, True)]),
                     zk + ["M"], [("ps", b)])
                P.op("act", I_act(ostg[k % 2], ps[b][:], AF.Copy, scale=sc), [("ps", b)], [("ostg", k % 2)])
                P.dma("pool", I_dma(pt["fourT"][128 * g:128 * g + 128, 512 * tb:512 * tb + 512], ostg[k % 2]), [("ostg", k % 2)],
                      [("fourT", p, g, tb)])
                k += 1
        self.bg(8)
        ar.pop()

    def phase_C(self, l, pt):
        P, ar, ps, s, d = self.P, self.ar, self.ps, self.scr, self.din
        p = pt["name"]
        ar.push()
        wout = ar.alloc([16, D], BF16)
        pw = ar.alloc([4, 128], BF16)
        psc = ar.alloc([4], F32)
        gpb = ar.alloc([D], F32)
        up = [ar.alloc([4, 528], BF16) for _ in range(2)]
        rc = [ar.alloc([4, 512], F32) for _ in range(2)]
        sA = ar.alloc([528], F32); sB = ar.alloc([528], F32); sM = ar.alloc([512], F32)
        dT = [ar.alloc([4, 512], BF16) for _ in range(2)]
        hp = [ar.alloc([4, 512], BF16) for _ in range(2)]
        at = [ar.alloc([8, 512], BF16) for _ in range(2)]
        ft = [ar.alloc([4, 512], BF16) for _ in range(2)]
        xt = [ar.alloc([D], F32) for _ in range(2)]
        tmp = [ar.alloc([D], F32) for _ in range(2)]
        junk = ar.alloc([512], BF16)
        st = ar.alloc([16], F32)
        for h in range(4):
            self.need(("wq_out", l, h))
        wq = s[f"wq_out{l}"].ap().rearrange("p (k c) -> p k c", c=D)
        for h in range(4):
            P.dma("sp", I_dma(wout[:, 4 * h:4 * h + 4, :], wq[:, 4 * h:4 * h + 4, :]), [("wq_out", l, h)], [("wout", h)])
        woutk = [("wout", h) for h in range(4)]
        P.dma("pool", I_dma(pw, d["pool_w"][l].rearrange("g c e -> c g e")), [], ["pw"])
        P.dma("sp", I_dmas([(psc, d["pscale"][l]), (gpb, d["gpost"][l].partition_broadcast(128))]), [], ["cvec"])
        NTL = NT // 512

        def loads(i):
            T0 = 512 * i
            P.dma("sp", I_dma(up[i % 2], pt["upe"].rearrange("(g c) t -> c g t", c=128)[:, :, T0:T0 + 528]),
                  ["all", ("upe_hl", p), ("upe_hr", p)], [("up", i % 2)])
            P.dma("sp", I_dmas([(rc[i % 2][:, g, :], pt["rcnt"][g:g + 1, T0:T0 + 512].partition_broadcast(128)) for g in range(4)]),
                  [], [("rc", i % 2)])
            P.dma("sp", I_dmas([(at[i % 2], pt["attnT"].rearrange("(h c) t -> c h t", c=128)[:, :, T0:T0 + 512]),
                                (ft[i % 2], pt["fourT"].rearrange("(g c) t -> c g t", c=128)[:, :, T0:T0 + 512])]),
                  ["all"], [("at", i % 2), ("ft", i % 2)])
        loads(0)
        xk = 0
        for i in range(NTL):
            T0 = 512 * i
            if i + 1 < NTL:
                loads(i + 1)
            u_, r_, d_, h_ = up[i % 2], rc[i % 2], dT[i % 2], hp[i % 2]
            upk, rck = ("up", i % 2), ("rc", i % 2)
            for g in range(4):
                ug = u_[:, g, :]
                P.op("dve", I_tt(sA[:, 1:528], ug[:, 0:527], ug[:, 1:528], ALU.add), [upk], ["sA"])
                cur, oth, lo, hi, stp = sA, sB, 1, 528, 1
                for k in range(g):
                    nlo, nhi = lo + stp, hi - stp
                    P.op("dve", I_tt(oth[:, nlo:nhi], cur[:, nlo - stp:nhi - stp], cur[:, nlo + stp:nhi + stp], ALU.add),
                         ["sA", "sB"], ["sA", "sB"])
                    cur, oth, lo, hi, stp = oth, cur, nlo, nhi, stp * 2
                P.op("dve", I_tt(sM, cur[:, 8:520], r_[:, g, :], ALU.mult), ["sA", "sB", rck], ["sM"])
                P.op("dve", I_tt(d_[:, g, :], sM, ug[:, 8:520], ALU.subtract), ["sM", upk], [("dT", i % 2, g)])
                b = 4 + g % 2
                P.op("pe", I_mm(ps[b][:], pw[:, g, :], d_[:, g, :]), ["pw", ("dT", i % 2, g)], [("ps", b)])
                P.op("act", I_act(h_[:, g, :], ps[b][:], AF.Copy, scale=psc[:, g:g + 1]), [("ps", b), "cvec"], [("hp", i % 2, g)])
            hk = [("hp", i % 2, g) for g in range(4)] + [("at", i % 2), ("ft", i % 2)]
            for s_ in range(4):
                r0 = T0 + 128 * s_
                x = xt[xk % 2]; tm = tmp[xk % 2]
                P.dma("sp", I_dma(x, pt["xsrc"][r0:r0 + 128, :]), ["all"], [("xt", xk % 2)])
                mm = []
                for kc in range(16):
                    if kc < 4:
                        lh = h_[:, kc, 128 * s_:128 * s_ + 128]
                    elif kc < 12:
                        lh = at[i % 2][:, kc - 4, 128 * s_:128 * s_ + 128]
                    else:
                        lh = ft[i % 2][:, kc - 12, 128 * s_:128 * s_ + 128]
                    for n in range(4):
                        mm.append((ps[n][:], lh, wout[:, kc, 512 * n:512 * n + 512], kc == 0, kc == 15))
                P.op("pe", I_mms(mm), hk + woutk, [("ps", n) for n in range(4)])
                self.tok_epilogue(ps, st, junk, tm, x, ("xt", xk % 2), ("tmp", xk % 2), gpb, 128)
                P.dma("pool", I_dma(pt["xme"][1 + r0:1 + r0 + 128, :], tm), [("tmp", xk % 2)], [("xme", p, i, s_)])
                xk += 1
            self.bg(3)
        ar.pop()

    def tok_epilogue(self, ps, st, junk, tm, x, xkey, tkey, gpb, rows):
        P = self.P
        for n in range(4):
            P.op("act", I_act(junk[0:rows], ps[n][0:rows, :], AF.Square, accum=st[0:rows, n:n + 1]), [("ps", n)], ["junk", ("st", n)])
        P.op("dve", I_tt(st[0:rows, 4:6], st[0:rows, 0:2], st[0:rows, 2:4], ALU.add), [("st", n) for n in range(4)], [("st", 4)])
        P.op("dve", I_tt(st[0:rows, 6:7], st[0:rows, 4:5], st[0:rows, 5:6], ALU.add), [("st", 4)], [("st", 6)])
        P.op("act", I_act(st[0:rows, 7:8], st[0:rows, 6:7], AF.Sqrt, scale=1.0 / D, bias=self.epsb[0:rows]), [("st", 6), "epsb"], [("st", 7)])
        P.op("dve", I_rcp(st[0:rows, 8:9], st[0:rows, 7:8]), [("st", 7)], [("st", 8)])
        for n in range(4):
            sl = slice(512 * n, 512 * n + 512)
            P.op("dve", I_stt(tm[0:rows, sl], ps[n][0:rows, :], st[0:rows, 8:9], gpb[0:rows, sl], ALU.mult, ALU.mult),
                 [("ps", n), ("st", 8), "cvec"], [(tkey, n)])
        P.op("dve", I_tt(tm[0:rows], tm[0:rows], x[0:rows], ALU.add), [(tkey, n) for n in range(4)] + [xkey], [tkey])

    def phase_D(self, l, pt):
        P, ar, ps, s, d = self.P, self.ar, self.ps, self.scr, self.din
        p = pt["name"]
        ar.push()
        NC = 458
        hT = ar.alloc([16, NC], BF16)
        aT = ar.alloc([NJ, 456], BF16)
        fg = ar.alloc([16, 512], F32)
        NWU, NWD = 3, 3
        wup = [ar.alloc([16, 256], BF16) for _ in range(NWU)]
        wdn = [ar.alloc([NJ, 128], BF16) for _ in range(NWD)]
        xt = [ar.alloc([D], F32) for _ in range(2)]
        xn = [ar.alloc([D], BF16)] * 2
        tmp = ar.alloc([D], F32)
        junk = tmp.bitcast(BF16)[:, 0:D]
        gcol = ar.alloc([16], F32)
        gpc = ar.alloc([16], F32)
        cw = ar.alloc([3 * 88], F32)
        cb = ar.alloc([88], F32)
        acc = [[ar.alloc([456], F32) for _ in range(2)] for _ in range(2)]
        gel = [ar.alloc([456], F32) for _ in range(2)]
        sq = [ar.alloc([512], F32) for _ in range(2)]
        sqa = ar.alloc([512], F32)
        st4 = ar.alloc([32], F32)
        P.dma("sp", I_dmas([(gcol, d["gffnT"][l]), (gpc, d["gpostfT"][l]), (cw, d["convw"][l]), (cb, d["convb"][l])]), [], ["cvec"])
        psT = [ps[2][:].bitcast(BF16), ps[3][:].bitcast(BF16)]
        xme = pt["xme"]
        wqu = s[f"wq_up{l}"].ap().rearrange("j p (k c) -> j p k c", c=256)
        wqd = s[f"wq_down{l}"].ap().rearrange("a p (j c) -> a p j c", c=128)
        kx = [0]
        ju = [0]
        jd = [0]
        a0 = 0
        for ti, n in enumerate(FFN_TILES):
            ncol = n + 2
            nblk = (n + 127) // 128
            subs = [(128 * m, min(128, n - 128 * m)) for m in range(nblk)] + [(-1, 2)]
            for (off, rows) in subs:
                k = kx[0]; kx[0] += 1
                x, xb = xt[k % 2], xn[k % 2]
                c0 = 4 * (k % 4)
                if off >= 0:
                    P.dma("sp", I_dma(x[0:rows], xme[1 + a0 + off:1 + a0 + off + rows, :]),
                          ["all", ("xme_hl", p), ("xme_hr", p)], [("xt", k % 2)])
                else:
                    P.dma("sp", I_dmas([(x[0:1], xme[a0:a0 + 1, :]), (x[1:2], xme[a0 + n + 1:a0 + n + 2, :])]),
                          ["all", ("xme_hl", p), ("xme_hr", p)], [("xt", k % 2)])
                P.op("act", I_act(junk[0:rows], x[0:rows], AF.Square, accum=st4[0:rows, c0:c0 + 1]), [("xt", k % 2)], ["tmp", ("st", k % 4)])
                P.op("act", I_act(st4[0:rows, c0 + 1:c0 + 2], st4[0:rows, c0:c0 + 1], AF.Sqrt, scale=1.0 / D, bias=self.epsb[0:rows]),
                     [("st", k % 4), "epsb"], [("st", k % 4)])
                P.op("dve", I_rcp(st4[0:rows, c0 + 2:c0 + 3], st4[0:rows, c0 + 1:c0 + 2]), [("st", k % 4)], [("st", k % 4)])
                P.op("act", I_act(xb[0:rows], x[0:rows], AF.Copy, scale=st4[0:rows, c0 + 2:c0 + 3]), [("xt", k % 2), ("st", k % 4)], [("xn", 0)])
                for b in range(2):
                    P.op("pe", I_trs([(psT[b][:, 128 * j:128 * j + 128], xb[:, 128 * (8 * b + j):128 * (8 * b + j) + 128])
                                      for j in range(8)], self.ident), [("xn", 0), "consts"], [("ps", 2 + b)])
                    for j in range(8):
                        kc = 8 * b + j
                        if off >= 0:
                            P.op("dve", I_ts(hT[:, kc, 1 + off:1 + off + rows], psT[b][:, 128 * j:128 * j + rows], gcol[:, kc:kc + 1], ALU.mult),
                                 [("ps", 2 + b), "cvec"], [("hT", off, kc)])
                        else:
                            P.op("dve", I_ts(hT[:, kc, 0:1], psT[b][:, 128 * j:128 * j + 1], gcol[:, kc:kc + 1], ALU.mult),
                                 [("ps", 2 + b), "cvec"], [("hT", -1, kc)])
                            P.op("dve", I_ts(hT[:, kc, n + 1:n + 2], psT[b][:, 128 * j + 1:128 * j + 2], gcol[:, kc:kc + 1], ALU.mult),
                                 [("ps", 2 + b), "cvec"], [("hT", -2, kc)])
            hTk = [("hT", off, kc) for (off, _r) in subs for kc in range(16)] + [("hT", -2, kc) for kc in range(16)]
            for j in range(NJ):
                self.need(("wq_up", l, j, 1))
                w = wup[ju[0] % NWU]; wk = ("wup", ju[0] % NWU)
                P.dma("sp", I_dma(w, wqu[j]), [("wq_up", l, j, 0), ("wq_up", l, j, 1)], [wk])
                bg_, bv_ = 4 + 2 * (ju[0] % 2), 5 + 2 * (ju[0] % 2)
                P.op("pe", I_mms([(ps[bg_][:, 0:ncol], w[:, kc, 0:128], hT[:, kc, 0:ncol], kc == 0, kc == 15) for kc in range(16)] +
                                 [(ps[bv_][:, 0:ncol], w[:, kc, 128:256], hT[:, kc, 0:ncol], kc == 0, kc == 15) for kc in range(16)]),
                     [wk] + hTk, [("ps", bg_), ("ps", bv_)])
                ag, av = acc[ju[0] % 2]
                ak = ("acc", ju[0] % 2)
                for (a_, b_, cidx) in ((ag, bg_, j), (av, bv_, NJ + j)):
                    w0, w1, w2 = cw[:, cidx:cidx + 1], cw[:, 88 + cidx:88 + cidx + 1], cw[:, 176 + cidx:176 + cidx + 1]
                    P.op("act", I_act(a_[:, 0:n], ps[b_][:, 1:n + 1], AF.Identity, scale=w1, bias=cb[:, cidx:cidx + 1]),
                         [("ps", b_), "cvec"], [(ak, cidx >= NJ)])
                    P.op("dve", I_stt(a_[:, 0:n], ps[b_][:, 0:n], w0, a_[:, 0:n], ALU.mult, ALU.add), [("ps", b_), (ak, cidx >= NJ), "cvec"],
                         [(ak, cidx >= NJ)])
                    P.op("dve", I_stt(a_[:, 0:n], ps[b_][:, 2:n + 2], w2, a_[:, 0:n], ALU.mult, ALU.add), [("ps", b_), (ak, cidx >= NJ), "cvec"],
                         [(ak, cidx >= NJ)])
                ge = gel[ju[0] % 2]
                P.op("act", I_act(ge[:, 0:n], ag[:, 0:n], AF.Gelu_apprx_tanh), [(ak, False)], [("gel", ju[0] % 2)])
                P.op("dve", I_tt(aT[:, j, 0:n], ge[:, 0:n], av[:, 0:n], ALU.mult), [("gel", ju[0] % 2), (ak, True)], [("aT", j)])
                ju[0] += 1
            atk = [("aT", j) for j in range(NJ)]
            for dc in range(16):
                self.need(("wq_down", l, dc))
                w = wdn[jd[0] % NWD]; wk = ("wdn", jd[0] % NWD)
                P.dma("sp", I_dma(w, wqd[dc]), [("wq_down", l, dc)], [wk])
                b = jd[0] % 2
                P.op("pe", I_mms([(ps[b][:, 0:n], w[:, j, :], aT[:, j, 0:n], j == 0, j == NJ - 1) for j in range(NJ)]), [wk] + atk, [("ps", b)])
                P.op("act", I_act(fg[:, dc, 0:n], ps[b][:, 0:n], AF.Copy, scale=gpc[:, dc:dc + 1]), [("ps", b), "cvec"], [("fg", dc)])
                sqt = sq[jd[0] % 2]
                if dc == 0:
                    P.op("act", I_act(sqa[:, 0:n], ps[b][:, 0:n], AF.Square), [("ps", b)], ["sqa"])
                else:
                    P.op("act", I_act(sqt[:, 0:n], ps[b][:, 0:n], AF.Square), [("ps", b)], [("sq", jd[0] % 2)])
                    P.op("dve", I_tt(sqa[:, 0:n], sqa[:, 0:n], sqt[:, 0:n], ALU.add), [("sq", jd[0] % 2), "sqa"], ["sqa"])
                jd[0] += 1
            P.op("pe", I_mms([(ps[4][:, m:m + 1], sqa[:, 128 * m:128 * m + 128], self.ones_f[:, 0:1], True, True) for m in range(nblk)]),
                 ["sqa", "consts"], [("ps", 4)])
            fgk = [("fg", dc) for dc in range(16)]
            for m in range(nblk):
                rows = min(128, n - 128 * m)
                k = kx[0]; kx[0] += 1
                x = xt[k % 2]
                P.dma("sp", I_dma(x[0:rows], xme[1 + a0 + 128 * m:1 + a0 + 128 * m + rows, :]), ["all"], [("xt", k % 2)])
                P.op("pe", I_trs([(ps[dc // 4][:, 128 * (dc % 4):128 * (dc % 4) + 128], fg[:, dc, 128 * m:128 * m + 128]) for dc in range(16)],
                                 self.ident_f), fgk + ["consts"], [("ps", 0), ("ps", 1), ("ps", 2), ("ps", 3)])
                c0 = 16 + 4 * (m % 4)
                P.op("act", I_act(st4[0:rows, c0:c0 + 1], ps[4][0:rows, m:m + 1], AF.Sqrt, scale=1.0 / D, bias=self.epsb[0:rows]),
                     [("ps", 4), "epsb"], [("st2", m % 4)])
                P.op("dve", I_rcp(st4[0:rows, c0 + 1:c0 + 2], st4[0:rows, c0:c0 + 1]), [("st2", m % 4)], [("st2", m % 4)])
                for nn in range(4):
                    sl = slice(512 * nn, 512 * nn + 512)
                    P.op("dve", I_stt(tmp[0:rows, sl], ps[nn][0:rows, :], st4[0:rows, c0 + 1:c0 + 2], x[0:rows, sl], ALU.mult, ALU.add),
                         [("ps", nn), ("st2", m % 4), ("xt", k % 2)], ["tmp"])
                P.dma("pool", I_dma(pt["xdst"][a0 + 128 * m:a0 + 128 * m + rows, :], tmp[0:rows]), ["tmp"], [("xdst", p, ti, m)])
            a0 += n
            self.bg(2)
        ar.pop()


_NC_CACHE = {}


def _host_inputs(inputs):
    c = _consts()
    f32 = lambda a: np.ascontiguousarray(np.asarray(a, dtype=np.float32))
    w = {k: f32(inputs[k]) for k in ("w_in", "w_out", "w_up", "w_down", "pool_w", "fourier_w")}

    def colT(v):
        v = f32(v)
        return np.ascontiguousarray(v.reshape(DEPTH, 16, 128).transpose(0, 2, 1))
    shared = dict(w)
    shared["gpreT"] = colT(inputs["g_pre_mix"]); shared["gffnT"] = colT(inputs["g_pre_ffn"])
    shared["gpostfT"] = colT(inputs["g_post_ffn"])
    shared["gpost"] = f32(inputs["g_post_mix"]).reshape(DEPTH, 1, D)
    qn = f32(inputs["q_norm"]); kn = f32(inputs["k_norm"])
    sw = np.concatenate([np.arange(64, 128), np.arange(0, 64)])
    shared["qn"] = np.ascontiguousarray(np.stack([qn, qn[:, sw], kn, kn[:, sw]], axis=-1))
    shared["pscale"] = np.ascontiguousarray(f32(inputs["pool_scale"]).reshape(DEPTH, 4, 128).transpose(0, 2, 1))
    cw = f32(inputs["conv_w"]).reshape(DEPTH, 3, 88, 128).transpose(0, 3, 1, 2)
    shared["convw"] = np.ascontiguousarray(cw.reshape(DEPTH, 128, 3 * 88))
    shared["convb"] = np.ascontiguousarray(f32(inputs["conv_b"]).reshape(DEPTH, 88, 128).transpose(0, 2, 1))
    for k in ("ident", "ones_bf", "ones_f", "psw", "dft128", "cdft", "ropeC_s", "ropeS_s", "rcnt_s"):
        shared[k] = c[k]
    shared["ident_f"] = np.eye(128, dtype=np.float32)
    shared["g_s"] = c["g_s"].reshape(32, 128 * 128)
    xs = np.asarray(inputs["x_sample"], dtype=np.float32)
    xp = np.asarray(inputs["x_prompt"], dtype=np.float32)
    maps = []
    for core in range(8):
        r = core % 4
        m = dict(shared)
        m["xs"] = np.ascontiguousarray(xs[core])
        m["xp"] = np.ascontiguousarray(xp[core // 4, NT * r:NT * r + NT])
        m["ropeC_p"] = c[f"ropeC_p{r}"]; m["ropeS_p"] = c[f"ropeS_p{r}"]; m["rcnt_p"] = c[f"rcnt_p{r}"]
        m["g_p"] = c[f"g_p{r}"].reshape(128, 128 * 128)
        maps.append(m)
    return maps


def kernel(**inputs):
    if "nc" not in _NC_CACHE:
        _NC_CACHE["nc"] = Builder().build()
    nc = _NC_CACHE["nc"]
    maps = _host_inputs(inputs)
    res = run_bass_kernel_spmd(nc, maps, core_ids=list(range(8)))
    ys = np.stack([np.asarray(res.results[c]["ys"], dtype=np.float32) for c in range(8)], axis=0)
    yp = np.stack([np.concatenate([np.asarray(res.results[4 * g + r]["yp"], dtype=np.float32) for r in range(4)], axis=0)
                   for g in range(2)], axis=0)
    return (yp, ys)
```
